# Optimizing a Trainium2 kernel written in Bass

```python
import jax
import jax.numpy as jnp
from jax import lax

D_MODEL = 1024
BATCH = 16
SEQ = 2048
DEPTH = 4

N_MIXERS = 4
N_CONV_LAYERS = (DEPTH + 3) // 4
N_HGRN_LAYERS = (DEPTH + 2) // 4
N_MLSTM_LAYERS = (DEPTH + 1) // 4
N_SB_LAYERS = DEPTH // 4
D_FF = 2816
FFN_CONV = 3
CONV_WIDTH = 31
HGRN_EXPAND = 128
HGRN_HEADS = D_MODEL // HGRN_EXPAND
HGRN_FDIM = HGRN_HEADS * HGRN_EXPAND
HGRN_VDIM = D_MODEL // HGRN_HEADS
HGRN_CHUNK = 32
MLSTM_INNER = 2 * D_MODEL
MLSTM_HEADS = 4
MLSTM_HDIM = MLSTM_INNER // MLSTM_HEADS
MLSTM_CONV = 4
MLSTM_QKV_BLOCK = 4
MLSTM_CHUNK = 64
SB_HEADS = 16
SB_HDIM = D_MODEL // SB_HEADS
SB_BLOCK = 128
DN_ALPHA = (2.0 * DEPTH) ** 0.25
DN_BETA = (8.0 * DEPTH) ** -0.25
LN_EPS = 1e-5
RMS_EPS = 1e-6

kernel_name = 'hybrid_interleaved_conditioned_trunk'


def layer_norm(x, g, b):
    xf = x.astype(jnp.float32)
    mu = jnp.mean(xf, axis=-1, keepdims=True)
    var = jnp.mean(jnp.square(xf - mu), axis=-1, keepdims=True)
    y = (xf - mu) * lax.rsqrt(var + LN_EPS)
    return (y * g.astype(jnp.float32) + b.astype(jnp.float32)).astype(x.dtype)


def causal_dwconv(x, w, b):
    width = w.shape[0]
    xp = jnp.pad(x, ((0, 0), (width - 1, 0), (0, 0)))
    y = lax.conv_general_dilated(xp, w[:, None, :].astype(x.dtype), window_strides=(1,), padding='VALID',
                                 dimension_numbers=('NWC', 'WIO', 'NWC'), feature_group_count=x.shape[-1])
    return y + b


def adaln(c, w, b):
    m = jax.nn.silu(c) @ w + b
    return [t[:, None, :] for t in jnp.split(m, 6, axis=-1)]


def conformer_conv_module(h, w_pw1, b_pw1, dw_w, dw_b, ln_g, ln_b, w_pw2, b_pw2):
    u = jax.nn.glu(h @ w_pw1 + b_pw1, axis=-1)
    u = causal_dwconv(u, dw_w, dw_b)
    u = jax.nn.silu(layer_norm(u, ln_g, ln_b))
    return u @ w_pw2 + b_pw2


def hgrn2_chunked(q, k, v, log_f):
    B_, S_, H, dk = q.shape
    dv = v.shape[-1]
    C = HGRN_CHUNK
    N = S_ // C

    def to_chunks(t):
        return jnp.moveaxis(t.reshape(B_, N, C, H, t.shape[-1]), 1, 0)

    q, k, v, log_f = map(to_chunks, (q, k, v, log_f))
    b = jnp.cumsum(log_f, axis=2)
    b_ref = b[:, :, C // 2 - 1:C // 2]
    scores = jnp.einsum('nbthd,nbshd->nbhts', q * jnp.exp(b - b_ref), k * jnp.exp(b_ref - b))
    causal = jnp.tril(jnp.ones((C, C), dtype=bool))
    o_intra = jnp.einsum('nbhts,nbshv->nbthv', jnp.where(causal, scores, 0.0), v)
    b_last = b[:, :, -1]
    q_dec = q * jnp.exp(b)
    k_dec = k * jnp.exp(b_last[:, :, None] - b)

    def step(S, inp):
        qd, kd, vc, dec = inp
        o = jnp.einsum('bthd,bhdv->bthv', qd, S)
        S = dec[..., None] * S + jnp.einsum('bshd,bshv->bhdv', kd, vc)
        return S, o

    S0 = jnp.zeros((B_, H, dk, dv), jnp.float32)
    _, o_inter = lax.scan(step, S0, (q_dec, k_dec, v, jnp.exp(b_last)))
    o = o_intra + o_inter
    return jnp.moveaxis(o, 0, 1).reshape(B_, S_, H, dv)


def hgrn2_layer(h, lower_bound, w_in, norm_g, w_out):
    B_, S_, _ = h.shape
    proj = h @ w_in
    q_pre, f_pre, i_val, g_out = jnp.split(proj, [HGRN_FDIM, 2 * HGRN_FDIM, 2 * HGRN_FDIM + D_MODEL], axis=-1)
    lb = lower_bound.astype(jnp.float32)
    q = jax.nn.silu(q_pre.astype(jnp.float32))
    f = lb + (1.0 - lb) * jax.nn.sigmoid(f_pre.astype(jnp.float32))
    k = 1.0 - f
    heads = lambda t: t.reshape(B_, S_, HGRN_HEADS, -1)
    o = hgrn2_chunked(heads(q), heads(k), heads(i_val.astype(jnp.float32)), heads(jnp.log(f)))
    o = o * lax.rsqrt(jnp.mean(jnp.square(o), axis=-1, keepdims=True) + RMS_EPS) * norm_g.astype(jnp.float32)
    o = o.reshape(B_, S_, D_MODEL).astype(h.dtype) * jax.nn.silu(g_out)
    return o @ w_out


def headwise(x, w):
    B_, S_, _ = x.shape
    xg = x.reshape(B_, S_, w.shape[0], w.shape[1])
    return jnp.einsum('bsgi,gio->bsgo', xg, w).reshape(B_, S_, -1)


def mlstm_chunked(q, k, v, i_pre, f_pre):
    B_, S_, H, d = q.shape
    C = MLSTM_CHUNK
    N = S_ // C
    k = k * (d ** -0.5)
    log_f = jax.nn.log_sigmoid(f_pre)

    def to_chunks(t):
        return jnp.moveaxis(t.reshape((B_, N, C) + t.shape[2:]), 1, 0)

    qc, kc, vc, li, lf = map(to_chunks, (q, k, v, i_pre, log_f))
    bc = jnp.cumsum(lf, axis=2)
    causal = jnp.tril(jnp.ones((C, C), dtype=bool))[None, :, :, None]

    def step(carry, inp):
        Cm, n, m = carry
        qt, kt, vt, it, bt = inp
        log_w = bt[:, :, None, :] - bt[:, None, :, :] + it[:, None, :, :]
        log_w = jnp.where(causal, log_w, -jnp.inf)
        log_inter = bt + m[:, None, :]
        m_t = jnp.maximum(jnp.max(log_w, axis=2), log_inter)
        s_qk = jnp.einsum('bthd,bshd->btsh', qt, kt) * jnp.exp(log_w - m_t[:, :, None, :])
        w_inter = jnp.exp(log_inter - m_t)
        num = jnp.einsum('btsh,bshd->bthd', s_qk, vt) + w_inter[..., None] * jnp.einsum('bthd,bhde->bthe', qt, Cm)
        den = jnp.sum(s_qk, axis=2) + w_inter * jnp.einsum('bthd,bhd->bth', qt, n)
        h_out = num / jnp.maximum(jnp.abs(den), jnp.exp(-m_t))[..., None]
        b_last = bt[:, -1]
        log_ws = b_last[:, None, :] - bt + it
        m_new = jnp.maximum(b_last + m, jnp.max(log_ws, axis=1))
        ws = jnp.exp(log_ws - m_new[:, None, :])
        dec = jnp.exp(b_last + m - m_new)
        kw = kt * ws[..., None]
        Cm = dec[..., None, None] * Cm + jnp.einsum('bshd,bshe->bhde', kw, vt)
        n = dec[..., None] * n + jnp.sum(kw, axis=1)
        return (Cm, n, m_new), h_out

    init = (jnp.zeros((B_, H, d, d), jnp.float32), jnp.zeros((B_, H, d), jnp.float32), jnp.zeros((B_, H), jnp.float32))
    _, h_all = lax.scan(step, init, (qc, kc, vc, li, bc))
    return jnp.moveaxis(h_all, 0, 1).reshape(B_, S_, H, d)


def mlstm_layer(h, w_up, conv_w, conv_b, w_q, w_k, w_v, w_gates, b_gates, norm_g, skip, w_down):
    B_, S_, _ = h.shape
    xm, z = jnp.split(h @ w_up, 2, axis=-1)
    xc = jax.nn.silu(causal_dwconv(xm, conv_w, conv_b))
    q = headwise(xc, w_q)
    k = headwise(xc, w_k)
    v = headwise(xm, w_v)
    gates = jnp.concatenate([q, k, v], axis=-1) @ w_gates + b_gates
    i_pre, f_pre = jnp.split(gates.astype(jnp.float32), 2, axis=-1)
    heads = lambda t: t.reshape(B_, S_, MLSTM_HEADS, MLSTM_HDIM).astype(jnp.float32)
    hc = mlstm_chunked(heads(q), heads(k), heads(v), i_pre, f_pre)
    mu = jnp.mean(hc, axis=-1, keepdims=True)
    var = jnp.mean(jnp.square(hc - mu), axis=-1, keepdims=True)
    hc = ((hc - mu) * lax.rsqrt(var + LN_EPS)).reshape(B_, S_, MLSTM_INNER).astype(h.dtype)
    hc = hc * norm_g + skip * xc
    return (hc * jax.nn.silu(z)) @ w_down


def stick_breaking_layer(h, w_qkv, w_out):
    B_, S_, _ = h.shape
    qkv = (h @ w_qkv).reshape(B_, S_, 3, SB_HEADS, SB_HDIM).astype(jnp.float32)
    q = qkv[:, :, 0] * (SB_HDIM ** -0.5)
    k = qkv[:, :, 1]
    v = qkv[:, :, 2]
    blocks = []
    for t0 in range(0, S_, SB_BLOCK):
        t1 = t0 + SB_BLOCK
        z = jnp.einsum('bthd,bshd->bhts', q[:, t0:t1], k[:, :t1])
        valid = jnp.arange(t1)[None, :] < jnp.arange(t0, t1)[:, None]
        neg_log_1mb = jnp.where(valid, jax.nn.softplus(z), 0.0)
        later = lax.cumsum(neg_log_1mb, axis=3, reverse=True) - neg_log_1mb
        a = jnp.where(valid, jnp.exp(jax.nn.log_sigmoid(z) - later), 0.0)
        blocks.append(jnp.einsum('bhts,bshd->bthd', a, v[:, :t1]))
    o = jnp.concatenate(blocks, axis=1).reshape(B_, S_, D_MODEL).astype(h.dtype)
    return o @ w_out


def conv_ffn(h, w_up, conv_w, conv_b, w_down):
    u = causal_dwconv(h @ w_up, conv_w, conv_b)
    gate, val = jnp.split(u, 2, axis=-1)
    return (jax.nn.silu(gate) * val) @ w_down


def setup_inputs(seed: int = 0) -> dict:
    key = jax.random.key(seed)
    ks = iter(jax.random.split(key, 48))

    def nrm(shape, scale):
        return scale * jax.random.normal(next(ks), shape, jnp.float32)

    D, F = D_MODEL, D_FF
    NA, NB, NC, ND = N_CONV_LAYERS, N_HGRN_LAYERS, N_MLSTM_LAYERS, N_SB_LAYERS
    G = MLSTM_INNER // MLSTM_QKV_BLOCK
    blk = MLSTM_QKV_BLOCK
    b_gates = jnp.concatenate([nrm((NC, MLSTM_HEADS), 0.1),
                               jnp.linspace(3.0, 6.0, MLSTM_HEADS)[None, :] + nrm((NC, MLSTM_HEADS), 0.1)], axis=-1)
    return {
        'x': nrm((BATCH, SEQ, D), 1.0),
        'c': nrm((BATCH, D), 1.0),
        'ada_w': nrm((DEPTH, D, 6 * D), 0.1 * D ** -0.5),
        'ada_b': nrm((DEPTH, 6 * D), 0.02),
        'post_ln_g': 1.0 + nrm((DEPTH, 2, D), 0.05),
        'post_ln_b': nrm((DEPTH, 2, D), 0.02),
        'ffn_w_up': nrm((DEPTH, D, 2 * F), D ** -0.5),
        'ffn_conv_w': nrm((DEPTH, FFN_CONV, 2 * F), FFN_CONV ** -0.5),
        'ffn_conv_b': nrm((DEPTH, 2 * F), 0.02),
        'ffn_w_down': nrm((DEPTH, F, D), DN_BETA * F ** -0.5),
        'cc_w_pw1': nrm((NA, D, 2 * D), D ** -0.5),
        'cc_b_pw1': nrm((NA, 2 * D), 0.02),
        'cc_dw_w': nrm((NA, CONV_WIDTH, D), CONV_WIDTH ** -0.5),
        'cc_dw_b': nrm((NA, D), 0.02),
        'cc_ln_g': 1.0 + nrm((NA, D), 0.05),
        'cc_ln_b': nrm((NA, D), 0.02),
        'cc_w_pw2': nrm((NA, D, D), DN_BETA * D ** -0.5),
        'cc_b_pw2': nrm((NA, D), 0.02),
        'hg_lb_logits': 1.0 + nrm((DEPTH, HGRN_FDIM), 0.5),
        'hg_w_in': nrm((NB, D, 2 * HGRN_FDIM + 2 * D), D ** -0.5),
        'hg_norm_g': 1.0 + nrm((NB, HGRN_VDIM), 0.05),
        'hg_w_out': nrm((NB, D, D), DN_BETA * D ** -0.5),
        'ml_w_up': nrm((NC, D, 2 * MLSTM_INNER), D ** -0.5),
        'ml_conv_w': nrm((NC, MLSTM_CONV, MLSTM_INNER), MLSTM_CONV ** -0.5),
        'ml_conv_b': nrm((NC, MLSTM_INNER), 0.02),
        'ml_w_q': nrm((NC, G, blk, blk), blk ** -0.5),
        'ml_w_k': nrm((NC, G, blk, blk), blk ** -0.5),
        'ml_w_v': nrm((NC, G, blk, blk), blk ** -0.5),
        'ml_w_gates': nrm((NC, 3 * MLSTM_INNER, 2 * MLSTM_HEADS), 0.1 * (3 * MLSTM_INNER) ** -0.5),
        'ml_b_gates': b_gates,
        'ml_norm_g': 1.0 + nrm((NC, MLSTM_INNER), 0.05),
        'ml_skip': 1.0 + nrm((NC, MLSTM_INNER), 0.05),
        'ml_w_down': nrm((NC, MLSTM_INNER, D), DN_BETA * MLSTM_INNER ** -0.5),
        'sb_w_qkv': nrm((ND, D, 3 * D), D ** -0.5),
        'sb_w_out': nrm((ND, D, D), DN_BETA * D ** -0.5),
    }


def reference(x, c, ada_w, ada_b, post_ln_g, post_ln_b,
              ffn_w_up, ffn_conv_w, ffn_conv_b, ffn_w_down,
              cc_w_pw1, cc_b_pw1, cc_dw_w, cc_dw_b, cc_ln_g, cc_ln_b, cc_w_pw2, cc_b_pw2,
              hg_lb_logits, hg_w_in, hg_norm_g, hg_w_out,
              ml_w_up, ml_conv_w, ml_conv_b, ml_w_q, ml_w_k, ml_w_v, ml_w_gates, ml_b_gates,
              ml_norm_g, ml_skip, ml_w_down,
              sb_w_qkv, sb_w_out):
    lb_all = jax.nn.softmax(hg_lb_logits.astype(jnp.float32), axis=0)
    lb_all = jnp.cumsum(lb_all, axis=0) - lb_all[:1]
    for i in range(DEPTH):
        kind, j = i % N_MIXERS, i // N_MIXERS
        shift1, scale1, gate1, shift2, scale2, gate2 = adaln(c, ada_w[i], ada_b[i])
        h = x * (1.0 + scale1) + shift1
        if kind == 0:
            y = conformer_conv_module(h, cc_w_pw1[j], cc_b_pw1[j], cc_dw_w[j], cc_dw_b[j],
                                      cc_ln_g[j], cc_ln_b[j], cc_w_pw2[j], cc_b_pw2[j])
        elif kind == 1:
            y = hgrn2_layer(h, lb_all[i], hg_w_in[j], hg_norm_g[j], hg_w_out[j])
        elif kind == 2:
            y = mlstm_layer(h, ml_w_up[j], ml_conv_w[j], ml_conv_b[j], ml_w_q[j], ml_w_k[j], ml_w_v[j],
                            ml_w_gates[j], ml_b_gates[j], ml_norm_g[j], ml_skip[j], ml_w_down[j])
        else:
            y = stick_breaking_layer(h, sb_w_qkv[j], sb_w_out[j])
        x = layer_norm(DN_ALPHA * x + (1.0 + gate1) * y, post_ln_g[i, 0], post_ln_b[i, 0])
        h = x * (1.0 + scale2) + shift2
        y = conv_ffn(h, ffn_w_up[i], ffn_conv_w[i], ffn_conv_b[i], ffn_w_down[i])
        x = layer_norm(DN_ALPHA * x + (1.0 + gate2) * y, post_ln_g[i, 1], post_ln_b[i, 1])
    return x
```

```python
import numpy as np
from contextlib import ExitStack
import concourse.bass as bass
import concourse.mybir as mybir
from concourse.alu_op_type import AluOpType as ALU
from concourse.bass_utils import run_bass_kernel_spmd

F32 = mybir.dt.float32
BF16 = mybir.dt.bfloat16
AF = mybir.ActivationFunctionType
AX = mybir.AxisListType

EPOCH = 30000
NSLOT = 12


class Tile:
    def __init__(self, ap, name=""):
        self.ap = ap
        self.name = name
        self.wtok = None
        self.rtoks = {}
        self.excl = False

    def __getitem__(self, key):
        return View(self, self.ap[key])

    @property
    def v(self):
        return View(self, self.ap)

    def sub(self, key, name=""):
        return Tile(self.ap[key], name or self.name)


class View:
    def __init__(self, tile, ap):
        self.tile = tile
        self.ap = ap

    def __getitem__(self, key):
        return View(self.tile, self.ap[key])

    def re(self, s, **kw):
        return View(self.tile, self.ap.rearrange(s, **kw))

    def bc(self, shape):
        return View(self.tile, self.ap.broadcast_to(shape))

    def bitcast(self, dt):
        return View(self.tile, self.ap.bitcast(dt))


def _v(x):
    if isinstance(x, Tile):
        return x.v
    return x


class Rec:
    ENG = ["pe", "act", "dve", "pool", "sp"]

    def __init__(self, nc, es):
        self.nc = nc
        self.es = es
        self.es_sem = es
        self.ops = {e: [] for e in self.ENG}
        self.cnt = {e: 0 for e in self.ENG}
        self.epoch = {e: 0 for e in self.ENG}
        self.waited = {e: {} for e in self.ENG}
        self.sems = {}
        self.slot_i = {e: 0 for e in self.ENG}
        self.slot_uses = {}
        self.final_toks = []
        self.nsem = 0

    def sem(self, key):
        if key not in self.sems:
            self.sems[key] = self.es_sem.enter_context(self.nc.semaphore("s%d" % self.nsem))
            self.nsem += 1
        return self.sems[key]

    def sbuf(self, shape, dt, name):
        self.nsem += 1
        name = "sb_%s_%d" % (name, self.nsem)
        t = self.es.enter_context(self.nc.sbuf_tensor(name, list(shape), dt))
        return Tile(t.ap() if hasattr(t, "ap") and callable(t.ap) else t, name)

    def psum(self, shape, dt, name):
        self.nsem += 1
        name = "ps_%s_%d" % (name, self.nsem)
        t = self.es.enter_context(self.nc.psum_tensor(name, list(shape), dt))
        tl = Tile(t.ap() if hasattr(t, "ap") and callable(t.ap) else t, name)
        tl.excl = True
        return tl

    def dram(self, shape, dt, name, kind="Internal"):
        t = self.nc.dram_tensor(name, list(shape), dt, kind=kind)
        return Tile(t.ap(), name)

    def _collect(self, e, reads, writes):
        toks = {}

        def add(k, v):
            if toks.get(k, 0) < v:
                toks[k] = v

        for t in reads:
            if t.wtok is not None:
                add(*t.wtok)
            if t.excl:
                for k, v in t.rtoks.items():
                    add(k, v)
        for t in writes:
            if t.wtok is not None:
                add(*t.wtok)
            for k, v in t.rtoks.items():
                add(k, v)
        waits = []
        for k, v in toks.items():
            if e == "pe" and k[0] == "eng" and k[1] == "pe":
                continue
            if self.waited[e].get(k, 0) < v:
                self.waited[e][k] = v
                waits.append((k, v))
        return waits

    def _commit(self, tok, reads, writes):
        k, v = tok
        for t in reads:
            if t.excl:
                t.wtok = tok
                t.rtoks = {}
            elif t.rtoks.get(k, 0) < v:
                t.rtoks[k] = v
        for t in writes:
            t.wtok = tok
            t.rtoks = {}

    def op(self, e, fn, reads, writes):
        reads = [_v(r).tile for r in reads]
        writes = [_v(w).tile for w in writes]
        waits = self._collect(e, reads, writes)
        if self.cnt[e] >= EPOCH:
            self.cnt[e] = 0
            self.epoch[e] += 1
        self.cnt[e] += 1
        tok = (("eng", e, self.epoch[e]), self.cnt[e])
        self.sem(tok[0])
        for k, _ in waits:
            self.sem(k)
        self.ops[e].append((waits, fn, tok[0], 1))
        self._commit(tok, reads, writes)
        return tok

    def dma(self, q, out, in_, final=False, **kw):
        out = _v(out)
        in_ = _v(in_)
        reads = [in_.tile]
        writes = [out.tile]
        waits = self._collect(q, reads, writes)
        i = self.slot_i[q]
        self.slot_i[q] = (i + 1) % NSLOT
        key = ("dma", q, i)
        uses = self.slot_uses.get(key, 0)
        if uses > 0 and self.waited[q].get(key, 0) < 16 * uses:
            self.waited[q][key] = 16 * uses
            waits.append((key, 16 * uses))
        self.slot_uses[key] = uses + 1
        tok = (key, 16 * (uses + 1))
        self.sem(key)
        for k, _ in waits:
            self.sem(k)
        oa, ia = out.ap, in_.ap

        def fn(eng):
            return eng.dma_start(out=oa, in_=ia, **kw)

        self.ops[q].append((waits, fn, key, 16))
        self._commit(tok, reads, writes)
        if final:
            self.final_toks.append(tok)
        return tok

    def matmul(self, out, lhsT, rhs, start=True, stop=True, skip=False):
        out, lhsT, rhs = _v(out), _v(lhsT), _v(rhs)
        o, l, r = out.ap, lhsT.ap, rhs.ap
        return self.op("pe", lambda eng: eng.matmul(o, l, r, start=start, stop=stop, skip_group_check=skip),
                       [lhsT, rhs], [out])

    def transpose(self, out, in_, ident):
        out, in_, ident = _v(out), _v(in_), _v(ident)
        o, i, d = out.ap, in_.ap, ident.ap
        return self.op("pe", lambda eng: eng.transpose(o, i, d), [in_, ident], [out])

    def act(self, out, in_, func, bias=None, scale=None, extra_r=()):
        out, in_ = _v(out), _v(in_)
        o, i = out.ap, in_.ap
        reads = [in_] + list(extra_r)
        kw = {}
        if bias is not None:
            if isinstance(bias, (View, Tile)):
                bias = _v(bias)
                reads.append(bias)
                kw["bias"] = bias.ap
            else:
                kw["bias"] = bias
        if scale is not None:
            if isinstance(scale, (View, Tile)):
                scale = _v(scale)
                reads.append(scale)
                kw["scale"] = scale.ap
            else:
                kw["scale"] = scale
        return self.op("act", lambda eng: eng.activation(o, i, func, **kw), reads, [out])

    def tscalar(self, e, out, in0, s1, s2, op0, op1=None):
        out, in0 = _v(out), _v(in0)
        reads = [in0]

        def cv(s):
            if isinstance(s, (View, Tile)):
                s = _v(s)
                reads.append(s)
                return s.ap
            return s

        a1, a2 = cv(s1), cv(s2)
        o, i = out.ap, in0.ap
        if op1 is None:
            return self.op(e, lambda eng: eng.tensor_scalar(o, i, a1, None, op0), reads, [out])
        return self.op(e, lambda eng: eng.tensor_scalar(o, i, a1, a2, op0, op1), reads, [out])

    def ttensor(self, e, out, in0, in1, op):
        out, in0, in1 = _v(out), _v(in0), _v(in1)
        o, a, b = out.ap, in0.ap, in1.ap
        return self.op(e, lambda eng: eng.tensor_tensor(o, a, b, op), [in0, in1], [out])

    def stt(self, out, in0, scalar, in1, op0, op1):
        out, in0, in1 = _v(out), _v(in0), _v(in1)
        reads = [in0, in1]
        if isinstance(scalar, (View, Tile)):
            scalar = _v(scalar)
            reads.append(scalar)
            sa = scalar.ap
        else:
            sa = scalar
        o, a, b = out.ap, in0.ap, in1.ap
        return self.op("dve", lambda eng: eng.scalar_tensor_tensor(o, a, sa, b, op0, op1),
                       reads, [out])

    def copy(self, e, out, in_):
        out, in_ = _v(out), _v(in_)
        o, i = out.ap, in_.ap
        if e == "act":
            return self.op(e, lambda eng: eng.copy(o, i), [in_], [out])
        return self.op(e, lambda eng: eng.tensor_copy(o, i), [in_], [out])

    def memset(self, e, out, val):
        out = _v(out)
        o = out.ap
        return self.op(e, lambda eng: eng.memset(o, val), [], [out])

    def scan(self, out, d0, d1, initial, op0, op1):
        out, d0, d1 = _v(out), _v(d0), _v(d1)
        reads = [d0, d1]
        if isinstance(initial, (View, Tile)):
            initial = _v(initial)
            reads.append(initial)
            ia = initial.ap
        else:
            ia = initial
        o, a, b = out.ap, d0.ap, d1.ap
        return self.op("dve", lambda eng: eng.tensor_tensor_scan(o, a, b, ia, op0, op1),
                       reads, [out])

    def emit(self, final=False):
        nc = self.nc
        fin = []
        if final:
            for k, v in self.final_toks:
                fin.append((k, v))

        def replay(e, eng, tail=()):
            sems = self.sems
            for waits, fn, sk, inc in self.ops[e]:
                for k, v in waits:
                    eng.wait_ge(sems[k], v)
                fn(eng).then_inc(sems[sk], inc)
            for k, v in tail:
                eng.wait_ge(sems[k], v)

        with nc.Block() as block:
            @block.tensor
            def _(eng):
                replay("pe", eng)

            @block.scalar
            def _(eng):
                replay("act", eng)

            @block.vector
            def _(eng):
                replay("dve", eng)

            @block.gpsimd
            def _(eng):
                replay("pool", eng)

            @block.sync
            def _(eng):
                replay("sp", eng, fin)
        self.ops = {e: [] for e in self.ENG}


def new_nc():
    return bass.Bass("TRN2", target_bir_lowering=False)


D = 1024
S = 2048
NSEQ = 2
TOK = NSEQ * S
DFF = 2816
ALPHA = 8.0 ** 0.25
LN_EPS = 1e-5
RMS_EPS = 1e-6

COLSPEC = [("ada_b", 4 * 48), ("ln_g", 64), ("ln_b", 64), ("ffn_cw", 4 * 3 * 44), ("ffn_cb", 4 * 44),
           ("cc_b1", 16), ("cc_dw", 31 * 8), ("cc_dwb", 8), ("cc_lng", 8), ("cc_lnb", 8), ("cc_b2", 8),
           ("hg_lb", 32), ("hg_ng", 1), ("ml_cw", 64), ("ml_cb", 16), ("ml_ng", 16), ("ml_sk", 16)]
COLOFF = {}
_o = 0
for _n, _c in COLSPEC:
    COLOFF[_n] = _o
    _o += _c
NCOL = _o


def _tocols(a):
    a = np.ascontiguousarray(a, dtype=np.float32).reshape(-1, 128)
    return np.ascontiguousarray(a.T)


class Ctx:
    pass


def col(P, name, j, n=1):
    o = COLOFF[name] + j
    return P.cols[:, o:o + n]


def modc(P, l, j, c, b):
    return P.mod[:, l, j * 8 + c, b:b + 1]


def load_w(R, dst, src, kc, n, q="pool"):
    for k in range(kc):
        for n0 in range(0, n, 2048):
            n1 = min(n, n0 + 2048)
            R.dma(q, dst[:, k, n0:n1], src[k * 128:(k + 1) * 128, n0:n1])


def ln_core(R, P, z, T, gcol, bcol, outs, func):
    pm = P.ps_ln[0]
    pv = P.ps_ln[1]
    for c in range(8):
        R.matmul(pm[:, 0:T], P.onesD, z[c], start=(c == 0), stop=(c == 7))
    R.copy("act", P.ln_mean[:, 0:T], pm[:, 0:T])
    for c in range(8):
        R.ttensor("pool", z[c], z[c], P.ln_mean[:, 0:T], ALU.subtract)
        sq = P.ln_sq[c % 2]
        R.act(sq[:, 0:T], z[c], AF.Square)
        R.matmul(pv[:, 0:T], P.onesD, sq[:, 0:T], start=(c == 0), stop=(c == 7))
    R.act(P.ln_rstd[:, 0:T], pv[:, 0:T], AF.Ln, bias=P.eps_ln)
    R.act(P.ln_rstd[:, 0:T], P.ln_rstd[:, 0:T], AF.Exp, scale=-0.5)
    for c in range(8):
        R.ttensor("dve", z[c], z[c], P.ln_rstd[:, 0:T], ALU.mult)
        R.act(outs[c], z[c], func, scale=gcol(c), bias=bcol(c))


def load_x_tile(R, P, xin, xt, tok0, T):
    R.dma("sp", xt[:, :, 0:T], xin.v.re("(c p) t -> p c t", p=128)[:, :, tok0:tok0 + T])


def modulate(R, P, xt, h, l, jshift, b, T):
    for c in range(8):
        R.act(h[:, c, 0:T], xt[:, c, 0:T], AF.Identity,
              scale=modc(P, l, jshift + 1, c, b), bias=modc(P, l, jshift, c, b))


def resid_ln_store(R, P, xa, ysrc, l, sub, b, xout, tok0, T):
    jg = 2 if sub == 0 else 5
    for c in range(8):
        R.stt(xa[:, c, 0:T], ysrc(c), modc(P, l, jg, c, b), xa[:, c, 0:T], ALU.mult, ALU.add)
    z = [xa[:, c, 0:T] for c in range(8)]
    ln_core(R, P, z, T,
            lambda c: col(P, "ln_g", (l * 2 + sub) * 8 + c),
            lambda c: col(P, "ln_b", (l * 2 + sub) * 8 + c),
            z, AF.Identity)
    fin = xout is P.out_dram
    R.dma("sp", xout.v.re("(c p) t -> p c t", p=128)[:, :, tok0:tok0 + T], xa[:, :, 0:T], final=fin)


def setup_phase(R, P, es0):
    R.es = es0
    P.cols = R.sbuf([128, NCOL], F32, "cols")
    R.dma("sp", P.cols, P.cols_dram)
    P.mod = R.sbuf([128, 4, 48, 2], F32, "mod")
    P.onesD = R.sbuf([128, 128], F32, "onesD")
    R.memset("pool", P.onesD, 1.0 / D)
    P.eps_ln = R.sbuf([128, 1], F32, "eps_ln")
    R.memset("pool", P.eps_ln, LN_EPS)
    P.identf = R.sbuf([128, 128], F32, "identf")
    R.memset("pool", P.identf, 1.0)
    ia = P.identf.ap
    R.op("pool", lambda eng: eng.affine_select(ia, ia, [[-1, 128]], ALU.is_equal, 0.0, base=0,
                                               channel_multiplier=1), [P.identf], [P.identf])
    P.ident = R.sbuf([128, 128], BF16, "ident")
    R.copy("dve", P.ident, P.identf)
    P.ln_mean = R.sbuf([128, 512], F32, "ln_mean")
    P.ln_rstd = R.sbuf([128, 512], F32, "ln_rstd")
    P.ln_sq = [R.sbuf([128, 512], F32, "ln_sq%d" % i) for i in range(2)]


def adaln_phase(R, P):
    with ExitStack() as es:
        R.es = es
        cT = R.sbuf([128, 16], F32, "cT")
        sc = R.sbuf([128, 16], F32, "sc")
        bp1 = R.sbuf([128, 192], F32, "bp1")
        wb = [R.sbuf([128, 8, 512], F32, "adaw%d" % i) for i in range(2)]
        ps = [R.psum([128, 96], F32, "adaps%d" % i) for i in range(2)]
        R.dma("sp", cT, P.cT_dram)
        R.act(sc, cT, AF.Silu)
        R.copy("dve", bp1, col(P, "ada_b", 0, 192))
        for l in range(4):
            for (a, b_) in ((8, 24), (32, 48)):
                R.tscalar("dve", bp1[:, l * 48 + a:l * 48 + b_], bp1[:, l * 48 + a:l * 48 + b_], 1.0, None, ALU.add)
        for l in range(4):
            p = ps[l % 2]
            for g in range(12):
                w = wb[(l * 12 + g) % 2]
                src = P.ada_w[l].re("(k p) n -> p k n", p=128)[:, :, g * 512:(g + 1) * 512]
                R.dma("sp" if g % 2 == 0 else "act", w, src)
                for nn in range(4):
                    n = g * 4 + nn
                    for k in range(8):
                        R.matmul(p[:, n * 2:n * 2 + 2], w[:, k, nn * 128:(nn + 1) * 128],
                                 sc[:, k * 2:k * 2 + 2], start=(k == 0), stop=(k == 7))
            bv = View(bp1, bp1.ap[:, l * 48:(l + 1) * 48].unsqueeze(2).broadcast_to([128, 48, 2]))
            R.ttensor("dve", P.mod[:, l], p.v.re("p (n b) -> p n b", b=2), bv, ALU.add)
        R.emit()


def ffn_phase(R, P, l, xin, xout):
    T = 256
    NT = S // T
    with ExitStack() as es:
        R.es = es
        wup = R.sbuf([128, 8, 2 * DFF], BF16, "wup")
        wdn = R.sbuf([128, 22, D], BF16, "wdn")
        load_w(R, wup, P.ffn_w_up[l], 8, 2 * DFF)
        load_w(R, wdn, P.ffn_w_down[l], 22, D)
        xb = [R.sbuf([128, 8, T], F32, "fx%d" % i) for i in range(2)]
        hb = [R.sbuf([128, 8, T], BF16, "fh%d" % i) for i in range(2)]
        gb = [R.sbuf([128, 22, T], BF16, "fg%d" % i) for i in range(1)]
        ub = [R.sbuf([128, T + 2], F32, "fu%d" % i) for i in range(4)]
        acc = [R.sbuf([128, T], F32, "fa%d" % i) for i in range(4)]
        sg = [R.sbuf([128, T], F32, "fs%d" % i) for i in range(2)]
        yb = [R.sbuf([128, T], F32, "fy%d" % i) for i in range(2)]
        halo_all = R.sbuf([128, 44, 2], F32, "fhl")
        halo = [halo_all.sub((slice(None), n, slice(None))) for n in range(44)]
        pu = [R.psum([128, 512], F32, "fpu%d" % i) for i in range(4)]
        pd = [R.psum([128, 512], F32, "fpd%d" % i) for i in range(2)]
        P.ps_ln = [R.psum([128, 512], F32, "fpl%d" % i) for i in range(2)]
        it = 0
        for s in range(NSEQ):
            for n in range(44):
                R.memset("pool", halo[n], 0.0)
            for i in range(NT):
                tok0 = s * S + i * T
                xt, h, g = xb[it % 2], hb[it % 2], gb[0]
                it += 1
                load_x_tile(R, P, xin, xt, tok0, T)
                modulate(R, P, xt, h, l, 3, s, T)
                R.tscalar("pool", xt, xt, ALPHA, None, ALU.mult)
                for j in range(22):
                    res = []
                    for half in range(2):
                        n = j + 22 * half
                        idx = (2 * j + half) % 4
                        p, u, a = pu[idx], ub[idx], acc[idx]
                        for k in range(8):
                            R.matmul(p[:, 0:T], wup[:, k, n * 128:(n + 1) * 128], h[:, k, :],
                                     start=(k == 0), stop=(k == 7))
                        R.copy("pool", u[:, 0:2], halo[n])
                        R.copy("act", u[:, 2:T + 2], p[:, 0:T])
                        R.copy("pool", halo[n], u[:, T:T + 2])
                        cw = lambda k_: col(P, "ffn_cw", (l * 3 + k_) * 44 + n)
                        R.tscalar("dve", a, u[:, 2:T + 2], cw(2), col(P, "ffn_cb", l * 44 + n), ALU.mult, ALU.add)
                        R.stt(a, u[:, 1:T + 1], cw(1), a, ALU.mult, ALU.add)
                        R.stt(a, u[:, 0:T], cw(0), a, ALU.mult, ALU.add)
                        res.append(a)
                    sgt = sg[j % 2]
                    R.act(sgt, res[0], AF.Silu)
                    R.ttensor("pool", g[:, j, :], sgt, res[1], ALU.mult)
                for n in range(8):
                    p = pd[n % 2]
                    for k in range(22):
                        R.matmul(p[:, 0:T], wdn[:, k, n * 128:(n + 1) * 128], g[:, k, :],
                                 start=(k == 0), stop=(k == 21))
                    y = yb[n % 2]
                    R.copy("act", y, p[:, 0:T])
                    jg = 5
                    R.stt(xt[:, n, :], y, modc(P, l, jg, n, s), xt[:, n, :], ALU.mult, ALU.add)
                z = [xt[:, c, :] for c in range(8)]
                ln_core(R, P, z, T,
                        lambda c: col(P, "ln_g", (l * 2 + 1) * 8 + c),
                        lambda c: col(P, "ln_b", (l * 2 + 1) * 8 + c),
                        z, AF.Identity)
                fin = xout is P.out_dram
                R.dma("sp", xout.v.re("(c p) t -> p c t", p=128)[:, :, tok0:tok0 + T], xt, final=fin)
        R.emit()


def conformer_phase(R, P, l, xin, xout):
    T = 512
    NT = S // T
    with ExitStack() as es:
        R.es = es
        w1 = R.sbuf([128, 8, 2048], BF16, "cw1")
        w2 = R.sbuf([128, 8, 1024], BF16, "cw2")
        load_w(R, w1, P.cc_w_pw1, 8, 2048)
        load_w(R, w2, P.cc_w_pw2, 8, 1024)
        xb = [R.sbuf([128, 8, T], F32, "cx%d" % i) for i in range(2)]
        hb = [R.sbuf([128, 8, T], BF16, "ch%d" % i) for i in range(2)]
        uc = [R.sbuf([128, T + 30], F32, "cu%d" % i) for i in range(8)]
        acc = [R.sbuf([128, T], F32, "ca%d" % i) for i in range(8)]
        sig = [R.sbuf([128, T], F32, "csg%d" % i) for i in range(2)]
        h2 = [R.sbuf([128, T], BF16, "ch2_%d" % i) for i in range(8)]
        yb = [R.sbuf([128, T], F32, "cy%d" % i) for i in range(2)]
        pa = [R.psum([128, 512], F32, "cpa%d" % i) for i in range(2)]
        pg = [R.psum([128, 512], F32, "cpg%d" % i) for i in range(2)]
        py = [R.psum([128, 512], F32, "cpy%d" % i) for i in range(2)]
        P.ps_ln = [R.psum([128, 512], F32, "cpl%d" % i) for i in range(2)]
        it = 0
        for s in range(NSEQ):
            for c in range(8):
                R.memset("pool", uc[c][:, 0:30], 0.0)
            for i in range(NT):
                tok0 = s * S + i * T
                xt, h = xb[it % 2], hb[it % 2]
                it += 1
                load_x_tile(R, P, xin, xt, tok0, T)
                modulate(R, P, xt, h, l, 0, s, T)
                R.tscalar("pool", xt, xt, ALPHA, None, ALU.mult)
                for c in range(8):
                    a_, g_ = pa[c % 2], pg[c % 2]
                    for k in range(8):
                        R.matmul(a_, w1[:, k, c * 128:(c + 1) * 128], h[:, k, :], start=(k == 0), stop=(k == 7))
                    for k in range(8):
                        R.matmul(g_, w1[:, k, 1024 + c * 128:1024 + (c + 1) * 128], h[:, k, :],
                                 start=(k == 0), stop=(k == 7))
                    sg_ = sig[c % 2]
                    R.act(sg_, g_, AF.Sigmoid, bias=col(P, "cc_b1", 8 + c))
                    R.stt(uc[c][:, 30:30 + T], a_, col(P, "cc_b1", c), sg_, ALU.add, ALU.mult)
                    a = acc[c]
                    R.tscalar("dve", a, uc[c][:, 30:30 + T], col(P, "cc_dw", 30 * 8 + c), col(P, "cc_dwb", c),
                              ALU.mult, ALU.add)
                    for k in range(30):
                        R.stt(a, uc[c][:, k:k + T], col(P, "cc_dw", k * 8 + c), a, ALU.mult, ALU.add)
                    R.copy("pool", uc[c][:, 0:30], uc[c][:, T:T + 30])
                ln_core(R, P, [a.v for a in acc], T,
                        lambda c: col(P, "cc_lng", c), lambda c: col(P, "cc_lnb", c),
                        [t.v for t in h2], AF.Silu)
                ys = []
                for n in range(8):
                    p = py[n % 2]
                    for k in range(8):
                        R.matmul(p, w2[:, k, n * 128:(n + 1) * 128], h2[k], start=(k == 0), stop=(k == 7))
                    y = yb[n % 2]
                    R.act(y, p, AF.Identity, bias=col(P, "cc_b2", n))
                    R.stt(xt[:, n, :], y, modc(P, l, 2, n, s), xt[:, n, :], ALU.mult, ALU.add)
                z = [xt[:, c, :] for c in range(8)]
                ln_core(R, P, z, T,
                        lambda c: col(P, "ln_g", (l * 2) * 8 + c),
                        lambda c: col(P, "ln_b", (l * 2) * 8 + c),
                        z, AF.Identity)
                R.dma("sp", xout.v.re("(c p) t -> p c t", p=128)[:, :, tok0:tok0 + T], xt)
        R.emit()


WSPEC = [("ada_w", [4, D, 6 * D]), ("ffn_w_up", [4, D, 2 * DFF]), ("ffn_w_down", [4, DFF, D]),
         ("cc_w_pw1", [D, 2 * D]), ("cc_w_pw2", [D, D]),
         ("hg_w_in", [D, 4 * D]), ("hg_w_out", [D, D]),
         ("ml_w_up", [D, 4 * D]), ("ml_w_down", [2 * D, D]),
         ("ml_wq_bd", [16, 128, 128]), ("ml_wk_bd", [16, 128, 128]), ("ml_wv_bd", [16, 128, 128]),
         ("ml_w_gates", [6 * D, 8]), ("ml_bg_row", [128, 8]),
         ("sb_w_qkv", [D, 3 * D]), ("sb_w_out", [D, D])]

MIXERS = {}


def build(nlayers=4, mixer_only=False, dbg=False):
    nc = new_nc()
    with ExitStack() as es0:
        R = Rec(nc, es0)
        P = Ctx()
        P.xT = R.dram([D, TOK], F32, "xT", "ExternalInput")
        P.cT_dram = R.dram([128, 16], F32, "cT", "ExternalInput")
        P.cols_dram = R.dram([128, NCOL], F32, "cols", "ExternalInput")
        P.consts_dram = R.dram([128, NCONST], F32, "consts", "ExternalInput")
        for name, shp in WSPEC:
            setattr(P, name, R.dram(shp, F32, name, "ExternalInput"))
        P.out_dram = R.dram([D, TOK], F32, "out", "ExternalOutput")
        if dbg:
            P.dbg = R.dram([8, D, 1024], F32, "dbg", "ExternalOutput")
        xs = [R.dram([D, TOK], F32, "xsA"), R.dram([D, TOK], F32, "xsB")]
        P.hg_dram = R.dram([2 * D, TOK], BF16, "hg_scr")

        def checkpoint(idx, src):
            if not dbg or src is P.out_dram:
                return
            for j, t0 in enumerate((0, S - 256, S, 2 * S - 256)):
                R.dma("sp", P.dbg[idx][:, j * 256:(j + 1) * 256], src[:, t0:t0 + 256], final=True)
            R.emit()

        setup_phase(R, P, es0)
        setup_consts(R, P)
        adaln_phase(R, P)
        cur = P.xT
        for l in range(nlayers):
            last = (l == nlayers - 1)
            mo = P.out_dram if (last and mixer_only) else xs[0]
            MIXERS[l % 4](R, P, l, cur, mo)
            checkpoint(2 * l, mo)
            if last and mixer_only:
                break
            fo = P.out_dram if last else xs[1]
            ffn_phase(R, P, l, xs[0], fo)
            checkpoint(2 * l + 1, fo)
            cur = xs[1]
        R.emit(final=True)
    return nc


def host_prep(inputs):
    f = lambda a: np.ascontiguousarray(np.asarray(a, dtype=np.float32))
    cols = np.zeros((128, NCOL), np.float32)

    def put(name, arr):
        c = _tocols(arr)
        cols[:, COLOFF[name]:COLOFF[name] + c.shape[1]] = c

    put("ada_b", inputs["ada_b"])
    put("ln_g", inputs["post_ln_g"])
    put("ln_b", inputs["post_ln_b"])
    put("ffn_cw", inputs["ffn_conv_w"])
    put("ffn_cb", inputs["ffn_conv_b"])
    put("cc_b1", inputs["cc_b_pw1"])
    put("cc_dw", inputs["cc_dw_w"])
    put("cc_dwb", inputs["cc_dw_b"])
    put("cc_lng", inputs["cc_ln_g"])
    put("cc_lnb", inputs["cc_ln_b"])
    put("cc_b2", inputs["cc_b_pw2"])
    put("hg_lb", inputs["hg_lb_logits"])
    put("hg_ng", inputs["hg_norm_g"])
    put("ml_cw", inputs["ml_conv_w"])
    put("ml_cb", inputs["ml_conv_b"])
    put("ml_ng", inputs["ml_norm_g"])
    put("ml_sk", inputs["ml_skip"])

    def blockdiag(w):
        w = f(w)[0]
        out = np.zeros((16, 128, 128), np.float32)
        for c in range(16):
            for g in range(32):
                out[c, 4 * g:4 * g + 4, 4 * g:4 * g + 4] = w[c * 32 + g]
        return out

    shared = {
        "cols": cols, "consts": make_consts(),
        "ada_w": f(inputs["ada_w"]), "ffn_w_up": f(inputs["ffn_w_up"]), "ffn_w_down": f(inputs["ffn_w_down"]),
        "cc_w_pw1": f(inputs["cc_w_pw1"])[0], "cc_w_pw2": f(inputs["cc_w_pw2"])[0],
        "hg_w_in": f(inputs["hg_w_in"])[0], "hg_w_out": f(inputs["hg_w_out"])[0],
        "ml_w_up": f(inputs["ml_w_up"])[0], "ml_w_down": f(inputs["ml_w_down"])[0],
        "ml_wq_bd": blockdiag(inputs["ml_w_q"]), "ml_wk_bd": blockdiag(inputs["ml_w_k"]),
        "ml_wv_bd": blockdiag(inputs["ml_w_v"]),
        "ml_w_gates": f(inputs["ml_w_gates"])[0],
        "ml_bg_row": np.ascontiguousarray(np.broadcast_to(f(inputs["ml_b_gates"])[0][None, :], (128, 8))),
        "sb_w_qkv": f(inputs["sb_w_qkv"])[0], "sb_w_out": f(inputs["sb_w_out"])[0],
    }
    return shared


def core_inputs(inputs, shared, core):
    x = np.asarray(inputs["x"], dtype=np.float32)
    c = np.asarray(inputs["c"], dtype=np.float32)
    b0 = core * NSEQ
    xT = np.ascontiguousarray(x[b0:b0 + NSEQ].reshape(TOK, D).T)
    cc = c[b0:b0 + NSEQ]
    cT = np.ascontiguousarray(cc.reshape(NSEQ, 8, 128).transpose(2, 1, 0).reshape(128, 16))
    m = dict(shared)
    m["xT"] = xT
    m["cT"] = cT
    return m


_NC_CACHE = {}


def run_cores(inputs, cores, nlayers=4, mixer_only=False, dbg=False):
    key = (nlayers, mixer_only, dbg)
    if key not in _NC_CACHE:
        _NC_CACHE[key] = build(nlayers, mixer_only, dbg)
    nc = _NC_CACHE[key]
    shared = host_prep(inputs)
    in_maps = [core_inputs(inputs, shared, c) for c in cores]
    res = run_bass_kernel_spmd(nc, in_maps, core_ids=list(range(len(cores))))
    outs = []
    for r in res.results:
        o = np.asarray(r["out"], dtype=np.float32)
        outs.append(np.ascontiguousarray(o.T).reshape(NSEQ, S, D))
    out = np.concatenate(outs, axis=0)
    if dbg:
        return out, np.asarray(res.results[0]["dbg"])
    return out


def kernel(**inputs):
    out = run_cores(inputs, list(range(8)))
    return out.astype(np.float32)


MIXERS[0] = conformer_phase


CONSTSPEC = [("tri128", 128), ("tril128", 128), ("triu_s128", 128), ("bd32", 128), ("cmask", 4),
             ("scanmask", 256), ("sbmask", 4 * 512)]
CONSTOFF = {}
_o = 0
for _n, _c in CONSTSPEC:
    CONSTOFF[_n] = _o
    _o += _c
NCONST = _o


def make_consts():
    c = np.zeros((128, NCONST), np.float32)
    p = np.arange(128)[:, None]
    f = np.arange(128)[None, :]

    def put(name, a):
        c[:, CONSTOFF[name]:CONSTOFF[name] + a.shape[1]] = a

    put("tri128", (p <= f).astype(np.float32))
    put("tril128", (f <= p).astype(np.float32))
    put("triu_s128", (p < f).astype(np.float32))
    put("bd32", ((p <= f) & (p // 32 == f // 32)).astype(np.float32))
    put("cmask", (p // 32 == np.arange(4)[None, :]).astype(np.float32))
    sm = np.ones((128, 256), np.float32)
    sm[:, ::32] = 0.0
    put("scanmask", sm)
    ff = np.arange(512)[None, :]
    sb = np.concatenate([((r * 128 + p) < ff).astype(np.float32) for r in range(4)], axis=1)
    put("sbmask", sb)
    return c


def cst(P, name, n, off=0):
    return P.cm[name][:, off:off + n]


def setup_consts(R, P):
    P.ones128 = R.sbuf([128, 128], F32, "ones128")
    R.memset("pool", P.ones128, 1.0 / 128)
    P.onesf = R.sbuf([128, 128], F32, "onesf")
    R.memset("pool", P.onesf, 1.0)
    P.ones_bf = R.sbuf([128, 2], BF16, "onesbf")
    R.memset("pool", P.ones_bf, 1.0)
    P.eps_rms = R.sbuf([128, 1], F32, "eps_rms")
    R.memset("pool", P.eps_rms, RMS_EPS)
    P.one_col = R.sbuf([128, 1], F32, "one_col")
    R.memset("pool", P.one_col, 1.0)


def load_consts(R, P, names):
    P.cm = {}
    sizes = dict(CONSTSPEC)
    for nm in names:
        t = R.sbuf([128, sizes[nm]], F32, "c_" + nm)
        R.dma("sp", t, P.consts_dram[:, CONSTOFF[nm]:CONSTOFF[nm] + sizes[nm]])
        P.cm[nm] = t


def hgrn_phase(R, P, l, xin, xout):
    T = 256
    NT = S // T
    NJ = T // 128
    NCH = T // 32
    with ExitStack() as es:
        R.es = es
        load_consts(R, P, ["bd32", "cmask", "scanmask"])
        win = R.sbuf([128, 8, 4096], BF16, "hwin")
        wout = R.sbuf([128, 8, 1024], BF16, "hwout")
        load_w(R, win, P.hg_w_in, 8, 4096)
        load_w(R, wout, P.hg_w_out, 8, 1024)
        ex = R.sbuf([128, 32], F32, "hex")
        lbt = R.sbuf([128, 8], F32, "hlb")
        oml = R.sbuf([128, 8], F32, "homl")
        den = R.sbuf([128, 8], F32, "hden")
        R.act(ex, col(P, "hg_lb", 0, 32), AF.Exp)
        R.ttensor("dve", den, ex[:, 0:8], ex[:, 8:16], ALU.add)
        R.ttensor("dve", den, den, ex[:, 16:24], ALU.add)
        R.ttensor("dve", den, den, ex[:, 24:32], ALU.add)
        R.op("dve", (lambda o, i: (lambda eng: eng.reciprocal(o, i)))(den.ap, den.ap), [den], [den])
        R.memset("dve", lbt, 0.0)
        for r in range(1, l + 1):
            R.ttensor("dve", lbt, lbt, ex[:, r * 8:(r + 1) * 8], ALU.add)
        R.ttensor("dve", lbt, lbt, den, ALU.mult)
        R.tscalar("dve", oml, lbt, -1.0, 1.0, ALU.mult, ALU.add)

        xb = [R.sbuf([128, 8, T], F32, "hx%d" % i) for i in range(2)]
        hb = R.sbuf([128, 8, T], BF16, "hh")
        vtm = R.sbuf([128, NJ, 1024], BF16, "hv")
        qs = [R.sbuf([128, T], BF16, "hqs%d" % i) for i in range(8)]
        ks = [R.sbuf([128, T], BF16, "hks%d" % i) for i in range(8)]
        qd = [R.sbuf([128, T], BF16, "hqd%d" % i) for i in range(8)]
        kdm = [R.sbuf([128, NJ, 4, 128], BF16, "hkdm%d" % i) for i in range(8)]
        dec = [R.sbuf([128, NCH], F32, "hdec%d" % i) for i in range(8)]
        gsl = [R.sbuf([128, T], F32, "hgs%d" % i) for i in range(8)]
        osb = [R.sbuf([128, T], F32, "hos%d" % i) for i in range(8)]
        S32 = [R.sbuf([128, 128], F32, "hS%d" % i) for i in range(8)]
        Sbf = [R.sbuf([128, 128], BF16, "hSb%d" % i) for i in range(8)]
        og = R.sbuf([128, 8, T], BF16, "hog")
        NS = 2
        qf = [R.sbuf([128, T], F32, "hqf%d" % i) for i in range(NS)]
        ff = [R.sbuf([128, T], F32, "hff%d" % i) for i in range(NS)]
        kf = [R.sbuf([128, T], F32, "hkf%d" % i) for i in range(NS)]
        lf = [R.sbuf([128, T], F32, "hlf%d" % i) for i in range(NS)]
        bb = [R.sbuf([128, T], F32, "hbb%d" % i) for i in range(NS)]
        d1 = [R.sbuf([128, T], F32, "hd1%d" % i) for i in range(NS)]
        aa = [R.sbuf([128, T], F32, "haa%d" % i) for i in range(4)]
        kdT = [R.sbuf([128, T], BF16, "hkdT%d" % i) for i in range(NS)]
        pmk = [R.sbuf([128, 128], BF16, "hpm%d" % i) for i in range(4)]
        yb = [R.sbuf([128, T], F32, "hy%d" % i) for i in range(2)]
        P.ps_ln = [R.psum([128, 512], F32, "hpl%d" % i) for i in range(2)]
        pp = [R.psum([128, 512], F32, "hpp0"), P.ps_ln[1], P.ps_ln[0]]
        pt_all = R.psum([128, 2, 128], BF16, "hpt")
        pt = [pt_all[:, i, :] for i in range(2)]
        psc_all = R.psum([128, 4, 128], F32, "hpsc")
        psc = [psc_all[:, i, :] for i in range(2)]
        pdl_all = R.psum([128, 4, 128], F32, "hpdl")
        pdl = [pdl_all[:, i, :] for i in range(2)]
        po_b = [R.psum([128, 4, 128], F32, "hpo%d" % i) for i in range(2)]
        po = [po_b[i // 4][:, i % 4, :] for i in range(8)]
        smask = cst(P, "scanmask", T)
        it = 0
        na = 0
        for s in range(NSEQ):
            for hd in range(8):
                R.memset("pool", S32[hd], 0.0)
                R.memset("pool", Sbf[hd], 0.0)
            for i in range(NT):
                tok0 = s * S + i * T
                xt = xb[it % 2]
                it += 1
                load_x_tile(R, P, xin, xt, tok0, T)
                modulate(R, P, xt, hb, l, 0, s, T)
                R.tscalar("pool", xt, xt, ALPHA, None, ALU.mult)
                for jj in range(NJ):
                    for half in range(2):
                        p = pp[(jj * 2 + half) % 3]
                        for k in range(8):
                            R.matmul(p, hb[:, k, jj * 128:(jj + 1) * 128],
                                     win[:, k, 2048 + half * 512:2048 + (half + 1) * 512],
                                     start=(k == 0), stop=(k == 7))
                        R.copy("act", vtm[:, jj, half * 512:(half + 1) * 512], p)
                for hd in range(8):
                    sl = hd % NS
                    pq, pf, pg = pp[0], pp[1], pp[2]
                    for (p, cb) in ((pq, hd), (pf, 8 + hd), (pg, 24 + hd)):
                        for k in range(8):
                            R.matmul(p[:, 0:T], win[:, k, cb * 128:(cb + 1) * 128], hb[:, k, :],
                                     start=(k == 0), stop=(k == 7))
                    R.act(qf[sl], pq[:, 0:T], AF.Silu)
                    R.act(ff[sl], pf[:, 0:T], AF.Sigmoid)
                    R.act(gsl[hd], pg[:, 0:T], AF.Silu)
                    R.tscalar("dve", ff[sl], ff[sl], oml[:, hd:hd + 1], lbt[:, hd:hd + 1], ALU.mult, ALU.add)
                    R.act(lf[sl], ff[sl], AF.Ln)
                    R.tscalar("pool", kf[sl], ff[sl], -1.0, 1.0, ALU.mult, ALU.add)
                    R.scan(bb[sl], smask, lf[sl], 0.0, ALU.mult, ALU.add)
                    b3 = bb[sl].v.re("p (c j) -> p c j", j=32)
                    bref = View(bb[sl], b3.ap[:, :, 15:16].broadcast_to([128, NCH, 32]))
                    blast = View(bb[sl], b3.ap[:, :, 31:32].broadcast_to([128, NCH, 32]))
                    d13 = d1[sl].v.re("p (c j) -> p c j", j=32)
                    R.ttensor("dve", d13, b3, bref, ALU.subtract)
                    a1, a2, a3, a4 = aa[0], aa[1], aa[2], aa[3]
                    R.act(a1, d1[sl], AF.Exp)
                    R.ttensor("pool", qs[hd], qf[sl], a1, ALU.mult)
                    R.act(a2, d1[sl], AF.Exp, scale=-1.0)
                    R.ttensor("pool", ks[hd], kf[sl], a2, ALU.mult)
                    R.act(a3, bb[sl], AF.Exp)
                    R.ttensor("dve", qd[hd], qf[sl], a3, ALU.mult)
                    R.ttensor("dve", d13, blast, b3, ALU.subtract)
                    R.act(a4, d1[sl], AF.Exp)
                    R.ttensor("dve", kdT[sl], kf[sl], a4, ALU.mult)
                    R.act(dec[hd], View(bb[sl], b3.ap[:, :, 31]), AF.Exp)
                    for jj in range(NJ):
                        t_ = pt[jj % 2]
                        R.transpose(t_, kdT[sl][:, jj * 128:(jj + 1) * 128], P.ident)
                        for c in range(4):
                            R.act(kdm[hd][:, jj, c, :], t_, AF.Identity, scale=cst(P, "cmask", 1, c))
                for jj in range(NJ):
                    for hd in range(8):
                        sc_ = psc[hd % 2]
                        R.matmul(sc_, ks[hd][:, jj * 128:(jj + 1) * 128], qs[hd][:, jj * 128:(jj + 1) * 128])
                        pm_ = pmk[hd % 4]
                        R.ttensor("dve", pm_, sc_, cst(P, "bd32", 128), ALU.mult)
                        R.matmul(po[hd], vtm[:, jj, hd * 128:(hd + 1) * 128], pm_, start=(hd % 4 == 0), stop=False,
                                 skip=True)
                    for c in range(4):
                        for hd in range(8):
                            t0_ = jj * 128 + c * 32
                            R.matmul(po[hd][:, c * 32:(c + 1) * 32], Sbf[hd], qd[hd][:, t0_:t0_ + 32],
                                     start=False, stop=(c == 3), skip=True)
                            dl = pdl[hd % 2]
                            R.matmul(dl, kdm[hd][:, jj, c, :], vtm[:, jj, hd * 128:(hd + 1) * 128])
                            ch = jj * 4 + c
                            R.stt(S32[hd], S32[hd], dec[hd][:, ch:ch + 1], dl, ALU.mult, ALU.add)
                            R.copy("act", Sbf[hd], S32[hd])
                    for hd in range(8):
                        R.copy("act", osb[hd][:, jj * 128:(jj + 1) * 128], po[hd])
                for hd in range(8):
                    sq = P.ln_sq[hd % 2]
                    R.act(sq[:, 0:T], osb[hd], AF.Square)
                    pr = pp[hd % 3]
                    R.matmul(pr[:, 0:T], P.ones128, sq[:, 0:T])
                    rs = aa[hd % 4]
                    R.act(rs, pr[:, 0:T], AF.Ln, bias=P.eps_rms)
                    R.act(rs, rs, AF.Exp, scale=-0.5)
                    R.ttensor("dve", osb[hd], osb[hd], rs, ALU.mult)
                    R.stt(og[:, hd, :], osb[hd], col(P, "hg_ng", 0), gsl[hd], ALU.mult, ALU.mult)
                for n in range(8):
                    p = pp[n % 3]
                    for k in range(8):
                        R.matmul(p[:, 0:T], wout[:, k, n * 128:(n + 1) * 128], og[:, k, :],
                                 start=(k == 0), stop=(k == 7))
                    y = yb[n % 2]
                    R.copy("act", y, p[:, 0:T])
                    R.stt(xt[:, n, :], y, modc(P, l, 2, n, s), xt[:, n, :], ALU.mult, ALU.add)
                z = [xt[:, c, :] for c in range(8)]
                ln_core(R, P, z, T,
                        lambda c: col(P, "ln_g", (l * 2) * 8 + c),
                        lambda c: col(P, "ln_b", (l * 2) * 8 + c),
                        z, AF.Identity)
                R.dma("sp", xout.v.re("(c p) t -> p c t", p=128)[:, :, tok0:tok0 + T], xt)
        R.emit()


MIXERS[1] = hgrn_phase


def mlstm_phase(R, P, l, xin, xout):
    mlstm_a(R, P, l, xin)
    mlstm_b(R, P, l, xin, xout)


def mlstm_a(R, P, l, xin):
    T = 128
    NT = S // T
    DS = 512.0 ** -0.5
    with ExitStack() as es:
        R.es = es
        load_consts(R, P, ["tri128"])
        wup = R.sbuf([128, 8, 4096], BF16, "mwup")
        load_w(R, wup, P.ml_w_up, 8, 4096)
        bdq = R.sbuf([128, 16, 128], BF16, "mbdq")
        bdk = R.sbuf([128, 16, 128], BF16, "mbdk")
        bdv = R.sbuf([128, 16, 128], BF16, "mbdv")
        for dst, src in ((bdq, P.ml_wq_bd), (bdk, P.ml_wk_bd), (bdv, P.ml_wv_bd)):
            R.dma("pool", dst, src.v.re("c k n -> k c n"))
        wg = R.sbuf([128, 48, 8], BF16, "mwg")
        R.dma("pool", wg, P.ml_w_gates.v.re("(c p) g -> p c g", p=128))
        bgrow = R.sbuf([128, 8], F32, "mbg")
        R.dma("sp", bgrow, P.ml_bg_row)
        C32 = [[R.sbuf([128, 512], F32, "mC%d_%d" % (a, b)) for b in range(4)] for a in range(4)]
        Cbf = [[R.sbuf([128, 512], BF16, "mCb%d_%d" % (a, b)) for b in range(4)] for a in range(4)]
        n32 = R.sbuf([128, 16], F32, "mn32")
        nbf = R.sbuf([128, 16], BF16, "mnbf")
        xt = R.sbuf([128, 8, T], F32, "mx")
        hb = R.sbuf([128, 8, T], BF16, "mh")
        xm = R.sbuf([128, 16, T + 3], F32, "mxm")
        xmT = R.sbuf([128, 16, T], BF16, "mxmT")
        xcT = R.sbuf([128, 16, T], BF16, "mxcT")
        silz = R.sbuf([128, 16, T], BF16, "msilz")
        sxc = R.sbuf([128, 16, T], BF16, "msxc")
        qT = R.sbuf([128, 16, T], BF16, "mqT")
        kT = R.sbuf([128, 16, T], BF16, "mkT")
        vT = [R.sbuf([128, 4, T], BF16, "mvT%d" % i) for i in range(2)]
        vtm = R.sbuf([128, 2048], BF16, "mvtm")
        ktm = R.sbuf([128, 2048], BF16, "mktm")
        kw = [R.sbuf([128, 512], BF16, "mkw%d" % i) for i in range(2)]
        hc = P.ln_sq
        hn = [R.sbuf([128, 512], BF16, "mhn%d" % i) for i in range(2)]
        hg = R.sbuf([128, 16, T], BF16, "mhg")
        cacc = [R.sbuf([128, T], F32, "mca%d" % i) for i in range(2)]
        ep = [R.sbuf([128, T], F32, "mep%d" % i) for i in range(2)]
        spb = [R.sbuf([128, T], BF16, "msp%d" % i) for i in range(2)]
        gt = R.sbuf([128, 8], F32, "mgt")
        ee = R.sbuf([128, 4], F32, "mee")
        lfn = R.sbuf([128, 4], F32, "mlfn")
        t1 = R.sbuf([128, 4], F32, "mt1")
        uu = R.sbuf([128, 4], F32, "muu")
        eb = R.sbuf([128, 4], F32, "meb")
        dec = R.sbuf([128, 4], F32, "mdec")
        ws = R.sbuf([128, 4], F32, "mws")
        dn = [R.sbuf([128, 1], F32, "mdn%d" % i) for i in range(2)]
        fac = [R.sbuf([128, 1], F32, "mfac%d" % i) for i in range(2)]
        st6 = [R.sbuf([128, 6], F32, "mst%d" % i) for i in range(2)]
        mv = [R.sbuf([128, 2], F32, "mmv%d" % i) for i in range(2)]
        rs = [R.sbuf([128, 1], F32, "mrs%d" % i) for i in range(2)]
        pu = [R.psum([128, 4, 128], F32, "mpu%d" % i) for i in range(2)]
        pN = R.psum([128, 512], F32, "mpN")
        pU = [R.psum([128, 512], F32, "mpU%d" % i) for i in range(2)]
        psm = R.psum([128, 512], F32, "mpsm")
        pS = [psm[:, i * 128:(i + 1) * 128] for i in range(2)]
        pg = psm[:, 256:264]
        pb = psm[:, 264:268]
        pbl = psm[:, 268:272]
        pD = [psm[:, 272 + i:273 + i] for i in range(2)]
        pn_ = psm[:, 288:304]
        ptr = R.psum([128, 4, 128], BF16, "mptr")
        tri = cst(P, "tri128", 128)
        for s in range(NSEQ):
            R.memset("pool", xm[:, :, 0:3], 0.0)
            for a in range(4):
                for b in range(4):
                    R.memset("pool", C32[a][b], 0.0)
                    R.memset("pool", Cbf[a][b], 0.0)
            R.memset("pool", n32, 0.0)
            R.memset("pool", nbf, 0.0)
            for i in range(NT):
                tok0 = s * S + i * T
                load_x_tile(R, P, xin, xt, tok0, T)
                modulate(R, P, xt, hb, l, 0, s, T)
                for g4 in range(8):
                    p = pu[g4 % 2]
                    for cc in range(4):
                        n = g4 * 4 + cc
                        for k in range(8):
                            R.matmul(p[:, cc, :], wup[:, k, n * 128:(n + 1) * 128], hb[:, k, :],
                                     start=(k == 0), stop=(k == 7))
                    if g4 < 4:
                        R.copy("act", xm[:, g4 * 4:(g4 + 1) * 4, 3:T + 3], p)
                        R.copy("pool", xmT[:, g4 * 4:(g4 + 1) * 4, :], xm[:, g4 * 4:(g4 + 1) * 4, 3:T + 3])
                    else:
                        R.act(silz[:, (g4 - 4) * 4:(g4 - 3) * 4, :], p, AF.Silu)
                for c in range(16):
                    a = cacc[c % 2]
                    R.tscalar("dve", a, xm[:, c, 3:T + 3], col(P, "ml_cw", 3 * 16 + c), col(P, "ml_cb", c),
                              ALU.mult, ALU.add)
                    for k in range(3):
                        R.stt(a, xm[:, c, k:k + T], col(P, "ml_cw", k * 16 + c), a, ALU.mult, ALU.add)
                    R.act(xcT[:, c, :], a, AF.Silu)
                    R.tscalar("pool", sxc[:, c, :], xcT[:, c, :], col(P, "ml_sk", c), None, ALU.mult)
                R.copy("pool", xm[:, :, 0:3], xm[:, :, T:T + 3])
                for g4 in range(4):
                    p = pu[g4 % 2]
                    for cc in range(4):
                        c = g4 * 4 + cc
                        R.matmul(p[:, cc, :], bdq[:, c, :], xcT[:, c, :])
                    R.copy("act", qT[:, g4 * 4:(g4 + 1) * 4, :], p)
                for g4 in range(4):
                    p = pu[g4 % 2]
                    for cc in range(4):
                        c = g4 * 4 + cc
                        R.matmul(p[:, cc, :], bdk[:, c, :], xcT[:, c, :])
                    R.copy("act", kT[:, g4 * 4:(g4 + 1) * 4, :], p)
                for c in range(16):
                    R.matmul(pg, qT[:, c, :], wg[:, c, :], start=(c == 0), stop=False)
                for c in range(16):
                    R.matmul(pg, kT[:, c, :], wg[:, 16 + c, :], start=False, stop=False)
                for g4 in range(4):
                    p = pu[g4 % 2]
                    for cc in range(4):
                        c = g4 * 4 + cc
                        R.matmul(p[:, cc, :], bdv[:, c, :], xmT[:, c, :])
                    vt_ = vT[g4 % 2]
                    R.copy("act", vt_, p)
                    for cc in range(4):
                        c = g4 * 4 + cc
                        R.matmul(pg, vt_[:, cc, :], wg[:, 32 + c, :], start=False, stop=(c == 15))
                for g4 in range(4):
                    p = pu[g4 % 2]
                    for cc in range(4):
                        c = g4 * 4 + cc
                        R.matmul(p[:, cc, :], xmT[:, c, :], bdv[:, c, :])
                    R.copy("act", vtm[:, g4 * 512:(g4 + 1) * 512], p.v.re("p a b -> p (a b)"))
                for g4 in range(4):
                    p = pu[g4 % 2]
                    for cc in range(4):
                        c = g4 * 4 + cc
                        R.matmul(p[:, cc, :], xcT[:, c, :], bdk[:, c, :])
                    R.copy("act", ktm[:, g4 * 512:(g4 + 1) * 512], p.v.re("p a b -> p (a b)"))
                R.ttensor("dve", gt, pg, bgrow, ALU.add)
                R.act(ee, gt[:, 4:8], AF.Exp, scale=-1.0)
                R.act(lfn, ee, AF.Ln, bias=1.0)
                R.matmul(pb, tri, lfn)
                R.matmul(pbl, P.onesf, lfn)
                R.ttensor("dve", t1, gt[:, 0:4], pb, ALU.add)
                R.act(uu, t1, AF.Exp, bias=float(np.log(DS)))
                R.act(eb, pb, AF.Exp, scale=-1.0)
                R.act(dec, pbl, AF.Exp, scale=-1.0)
                R.ttensor("dve", ws, uu, dec, ALU.mult)
                for hh in range(4):
                    sl = hh % 2
                    for dc in range(4):
                        R.matmul(pS[sl], kT[:, hh * 4 + dc, :], qT[:, hh * 4 + dc, :], start=(dc == 0), stop=(dc == 3))
                    R.stt(spb[sl], pS[sl], uu[:, hh:hh + 1], tri, ALU.mult, ALU.mult)
                    R.matmul(pN, spb[sl], vtm[:, hh * 512:(hh + 1) * 512], start=True, stop=False)
                    for dc in range(4):
                        R.matmul(pN, qT[:, hh * 4 + dc, :], Cbf[hh][dc], start=False, stop=(dc == 3))
                    R.matmul(pD[sl], spb[sl], P.ones_bf[:, 0:1], start=True, stop=False)
                    for dc in range(4):
                        R.matmul(pD[sl], qT[:, hh * 4 + dc, :], nbf[:, hh * 4 + dc:hh * 4 + dc + 1],
                                 start=False, stop=(dc == 3))
                    R.act(dn[sl], pD[sl], AF.Abs, scale=eb[:, hh:hh + 1])
                    R.tscalar("dve", dn[sl], dn[sl], 1.0, None, ALU.max)
                    R.op("dve", (lambda o, i_: (lambda eng: eng.reciprocal(o, i_)))(dn[sl].ap, dn[sl].ap),
                         [dn[sl]], [dn[sl]])
                    R.ttensor("dve", fac[sl], dn[sl], eb[:, hh:hh + 1], ALU.mult)
                    R.act(hc[sl], pN, AF.Identity, scale=fac[sl])
                    R.op("dve", (lambda o, i_: (lambda eng: eng.bn_stats(o, i_)))(st6[sl].ap, hc[sl].ap),
                         [hc[sl]], [st6[sl]])
                    R.op("dve", (lambda o, i_: (lambda eng: eng.bn_aggr(o, i_)))(mv[sl].ap, st6[sl].ap),
                         [st6[sl]], [mv[sl]])
                    R.act(rs[sl], mv[sl][:, 1:2], AF.Ln, bias=P.eps_ln)
                    R.act(rs[sl], rs[sl], AF.Exp, scale=-0.5)
                    R.tscalar("dve", hn[sl], hc[sl], mv[sl][:, 0:1], rs[sl], ALU.subtract, ALU.mult)
                    for e4 in range(4):
                        R.transpose(ptr[:, e4, :], hn[sl][:, e4 * 128:(e4 + 1) * 128], P.ident)
                    for e4 in range(4):
                        c = hh * 4 + e4
                        e_ = ep[e4 % 2]
                        R.stt(e_, ptr[:, e4, :], col(P, "ml_ng", c), sxc[:, c, :], ALU.mult, ALU.add)
                        R.ttensor("pool", hg[:, c, :], e_, silz[:, c, :], ALU.mult)
                    R.tscalar("pool", kw[sl], ktm[:, hh * 512:(hh + 1) * 512], ws[:, hh:hh + 1], None, ALU.mult)
                    for dc in range(4):
                        u_ = pU[dc % 2]
                        R.matmul(u_, kw[sl][:, dc * 128:(dc + 1) * 128], vtm[:, hh * 512:(hh + 1) * 512])
                        R.stt(C32[hh][dc], C32[hh][dc], dec[:, hh:hh + 1], u_, ALU.mult, ALU.add)
                        R.copy("act", Cbf[hh][dc], C32[hh][dc])
                        cix = hh * 4 + dc
                        R.matmul(pn_[:, cix:cix + 1], kw[sl][:, dc * 128:(dc + 1) * 128], P.ones_bf[:, 0:1])
                        R.stt(n32[:, cix:cix + 1], n32[:, cix:cix + 1], dec[:, hh:hh + 1], pn_[:, cix:cix + 1],
                              ALU.mult, ALU.add)
                R.copy("act", nbf, n32)
                R.dma("sp", P.hg_dram.v.re("(c p) t -> p c t", p=128)[:, :, tok0:tok0 + T], hg)
        R.emit()


def mlstm_b(R, P, l, xin, xout):
    T = 512
    NT = S // T
    with ExitStack() as es:
        R.es = es
        wdn = R.sbuf([128, 16, 1024], BF16, "mwdn")
        load_w(R, wdn, P.ml_w_down, 16, 1024)
        xb = [R.sbuf([128, 8, T], F32, "mbx%d" % i) for i in range(2)]
        hgb = [R.sbuf([128, 16, T], BF16, "mbh%d" % i) for i in range(2)]
        yb = [R.sbuf([128, T], F32, "mby%d" % i) for i in range(2)]
        py = [R.psum([128, 512], F32, "mbp%d" % i) for i in range(2)]
        P.ps_ln = [R.psum([128, 512], F32, "mbl%d" % i) for i in range(2)]
        it = 0
        for s in range(NSEQ):
            for i in range(NT):
                tok0 = s * S + i * T
                xt, hg = xb[it % 2], hgb[it % 2]
                it += 1
                load_x_tile(R, P, xin, xt, tok0, T)
                R.dma("act", hg, P.hg_dram.v.re("(c p) t -> p c t", p=128)[:, :, tok0:tok0 + T])
                R.tscalar("pool", xt, xt, ALPHA, None, ALU.mult)
                for n in range(8):
                    p = py[n % 2]
                    for k in range(16):
                        R.matmul(p, wdn[:, k, n * 128:(n + 1) * 128], hg[:, k, :], start=(k == 0), stop=(k == 15))
                    y = yb[n % 2]
                    R.copy("act", y, p)
                    R.stt(xt[:, n, :], y, modc(P, l, 2, n, s), xt[:, n, :], ALU.mult, ALU.add)
                z = [xt[:, c, :] for c in range(8)]
                ln_core(R, P, z, T,
                        lambda c: col(P, "ln_g", (l * 2) * 8 + c),
                        lambda c: col(P, "ln_b", (l * 2) * 8 + c),
                        z, AF.Identity)
                R.dma("sp", xout.v.re("(c p) t -> p c t", p=128)[:, :, tok0:tok0 + T], xt)
        R.emit()


MIXERS[2] = mlstm_phase


def sb_phase(R, P, l, xin, xout):
    T = 512
    NQ = S // T
    with ExitStack() as es:
        R.es = es
        load_consts(R, P, ["tril128", "triu_s128"])
        tril = R.sbuf([128, 128], BF16, "trilbf")
        R.copy("dve", tril, cst(P, "tril128", 128))
        triu = R.sbuf([128, 128], BF16, "triubf")
        R.copy("dve", triu, cst(P, "triu_s128", 128))
        sbm = R.sbuf([128, 4 * 512], BF16, "sbm")
        R.dma("pool", sbm, P.consts_dram[:, CONSTOFF["sbmask"]:CONSTOFF["sbmask"] + 2048])
        wqkv = R.sbuf([128, 8, 3072], BF16, "swqkv")
        wout = R.sbuf([128, 8, 1024], BF16, "swout")
        load_w(R, wqkv, P.sb_w_qkv, 8, 3072)
        load_w(R, wout, P.sb_w_out, 8, 1024)
        kT = R.sbuf([128, 8, S], BF16, "skT")
        vtm = R.sbuf([128, 16, 1024], BF16, "svtm")
        xt = R.sbuf([128, 8, T], F32, "sx")
        hb = R.sbuf([128, 8, T], BF16, "sh")
        qT = R.sbuf([128, 8, T], BF16, "sqT")
        oT = R.sbuf([128, 8, T], BF16, "soT")
        E = [R.sbuf([128, T], F32, "sE%d" % i) for i in range(2)]
        SP = [R.sbuf([128, T], BF16, "sSP%d" % i) for i in range(2)]
        X = [R.sbuf([128, T], F32, "sX%d" % i) for i in range(2)]
        A = [R.sbuf([128, T], BF16, "sA%d" % i) for i in range(2)]
        osb = [R.sbuf([128, 4, 128], BF16, "sos%d" % i) for i in range(2)]
        yb = P.ln_sq
        zp = [R.psum([128, 512], F32, "szp%d" % i) for i in range(2)]
        acc = [R.psum([128, 512], F32, "sac%d" % i) for i in range(2)]
        po = R.psum([128, 4, 128], F32, "spo")
        ptr = R.psum([128, 4, 128], BF16, "sptr")
        P.ps_ln = [R.psum([128, 512], F32, "spl%d" % i) for i in range(2)]
        pp = [zp[0], zp[1], acc[0], acc[1]]
        ipp = 0
        for s in range(NSEQ):
            for Qb in range(NQ):
                tok0 = s * S + Qb * T
                load_x_tile(R, P, xin, xt, tok0, T)
                modulate(R, P, xt, hb, l, 0, s, T)
                R.tscalar("pool", xt, xt, ALPHA, None, ALU.mult)
                for c in range(8):
                    p = pp[ipp % 4]
                    ipp += 1
                    for k in range(8):
                        R.matmul(p, wqkv[:, k, c * 128:(c + 1) * 128], hb[:, k, :], start=(k == 0), stop=(k == 7))
                    R.act(qT[:, c, :], p, AF.Identity, scale=0.125)
                    p = pp[ipp % 4]
                    ipp += 1
                    for k in range(8):
                        R.matmul(p, wqkv[:, k, 1024 + c * 128:1024 + (c + 1) * 128], hb[:, k, :],
                                 start=(k == 0), stop=(k == 7))
                    R.copy("act", kT[:, c, Qb * T:(Qb + 1) * T], p)
                for qb in range(4):
                    for half in range(2):
                        p = pp[ipp % 4]
                        ipp += 1
                        for k in range(8):
                            R.matmul(p, hb[:, k, qb * 128:(qb + 1) * 128],
                                     wqkv[:, k, 2048 + half * 512:2048 + (half + 1) * 512],
                                     start=(k == 0), stop=(k == 7))
                        R.copy("act", vtm[:, Qb * 4 + qb, half * 512:(half + 1) * 512], p)
                nkb = 4 * Qb + 4
                for hp in range(8):
                    for kb in range(nkb - 1, -1, -1):
                        first = (kb == nkb - 1)
                        diag = kb >= 4 * Qb
                        for e in range(2):
                            R.matmul(zp[e], kT[e * 64:(e + 1) * 64, hp, kb * 128:(kb + 1) * 128],
                                     qT[e * 64:(e + 1) * 64, hp, :])
                        for e in range(2):
                            R.act(E[e], zp[e], AF.Exp)
                            if diag:
                                r = kb - 4 * Qb
                                R.ttensor("pool", E[e], E[e], sbm[:, r * 512:(r + 1) * 512], ALU.mult)
                            R.act(SP[e], E[e], AF.Ln, bias=1.0)
                        for e in range(2):
                            R.matmul(acc[e], tril, SP[e], start=first, stop=False, skip=True)
                        for e in range(2):
                            R.act(X[e], acc[e], AF.Exp, scale=-1.0)
                        for e in range(2):
                            R.matmul(acc[e], triu, SP[e], start=False, stop=(kb == 0), skip=True)
                        for e in range(2):
                            R.ttensor("dve", A[e], E[e], X[e], ALU.mult)
                        for e in range(2):
                            h_ = 2 * hp + e
                            for qb in range(max(0, kb - 4 * Qb), 4):
                                R.matmul(po[:, qb, e * 64:(e + 1) * 64], A[e][:, qb * 128:(qb + 1) * 128],
                                         vtm[:, kb, h_ * 64:(h_ + 1) * 64],
                                         start=(first and e == 0 and qb == 3), stop=(kb == 0), skip=True)
                    ob = osb[hp % 2]
                    R.copy("act", ob, po)
                    for qb in range(4):
                        R.transpose(ptr[:, qb, :], ob[:, qb, :], P.ident)
                    R.copy("dve", oT[:, hp, :], ptr.v.re("p a b -> p (a b)"))
                for n in range(8):
                    p = pp[ipp % 4]
                    ipp += 1
                    for k in range(8):
                        R.matmul(p, wout[:, k, n * 128:(n + 1) * 128], oT[:, k, :], start=(k == 0), stop=(k == 7))
                    y = yb[n % 2]
                    R.copy("act", y, p)
                    R.stt(xt[:, n, :], y, modc(P, l, 2, n, s), xt[:, n, :], ALU.mult, ALU.add)
                z = [xt[:, c, :] for c in range(8)]
                ln_core(R, P, z, T,
                        lambda c: col(P, "ln_g", (l * 2) * 8 + c),
                        lambda c: col(P, "ln_b", (l * 2) * 8 + c),
                        z, AF.Identity)
                R.dma("sp", xout.v.re("(c p) t -> p c t", p=128)[:, :, tok0:tok0 + T], xt)
        R.emit()


MIXERS[3] = sb_phase
```
